# Optimizing a Trainium2 kernel written in Bass

```python
import math
import jax
import jax.numpy as jnp
from jax import lax
import numpy as np

D_MODEL = 2048
BATCH = 16
SEQ = 256
DEPTH = 4
DEC_BATCH = 2
DEC_SEQ = 4096
PAST_LEN = 256

GRID_W = 64
BLOCK = 128
EPS = 1e-6
ROPE_BASE = 10000.0

HEAD_DIM = 128
A_HEADS = 4
A_KV_HEADS = 2
A_GROUP = A_HEADS // A_KV_HEADS
A_WINDOW = 128
B_HEADS = 4
B_HALF = HEAD_DIM // 2
C_HEADS = 16
C_HEAD_DIM = 64
C_INNER = C_HEADS * C_HEAD_DIM
C_GROUPS = 2
C_HPG = C_HEADS // C_GROUPS
C_STATE = 128
C_CONV = 5
C_CHUNK = 128
C_CONV_CH = C_INNER + 2 * C_GROUPS * C_STATE

A_Q = A_HEADS * HEAD_DIM
A_KV = A_KV_HEADS * HEAD_DIM
B_QKV = B_HEADS * HEAD_DIM
PROJ_SIZES = (A_Q, A_KV, A_KV, B_QKV, B_QKV, B_QKV, C_INNER, C_CONV_CH, 2 * C_HEADS)
IN_COLS = sum(PROJ_SIZES)
MIX_W = A_Q + B_QKV + C_INNER
FF = (8 * D_MODEL + 3 * 256 - 1) // (3 * 256) * 256
N_MOD = 6

kernel_name = 'hybrid_diffusion_prefix_trunk_step'


def rmsnorm(x, g):
    xf = x.astype(jnp.float32)
    y = xf * lax.rsqrt(jnp.mean(xf * xf, axis=-1, keepdims=True) + EPS)
    return (y * g.astype(jnp.float32)).astype(x.dtype)


def adaln(cond, w, bias):
    m = jax.nn.silu(cond) @ w + bias
    return jnp.split(m[..., None, :], N_MOD, axis=-1)


def split_projection(p):
    cuts, acc = [], 0
    for size in PROJ_SIZES[:-1]:
        acc += size
        cuts.append(acc)
    return jnp.split(p, cuts, axis=-1)


def axial_rope_tables(n, rot_dim):
    rows = n // GRID_W
    row = jnp.repeat(jnp.arange(rows), GRID_W).astype(jnp.float32)
    col = (jnp.arange(rows * GRID_W) % GRID_W).astype(jnp.float32)
    quarter = rot_dim // 4
    inv = ROPE_BASE ** (-jnp.arange(quarter, dtype=jnp.float32) / quarter)
    ang = jnp.concatenate([row[:, None] * inv, col[:, None] * inv], axis=-1)
    return jnp.cos(ang), jnp.sin(ang)


def apply_rope(x, cos, sin):
    half = x.shape[-1] // 2
    shape = (cos.shape[0],) + (1,) * (x.ndim - 3) + (half,)
    cos, sin = cos.reshape(shape), sin.reshape(shape)
    xf = x.astype(jnp.float32)
    x1, x2 = xf[..., :half], xf[..., half:]
    return jnp.concatenate([x1 * cos - x2 * sin, x1 * sin + x2 * cos], axis=-1).astype(x.dtype)


def sweep_query_blocks(fn, q):
    b, n = q.shape[:2]
    nb = n // BLOCK
    qb = jnp.moveaxis(q.reshape((b, nb, BLOCK) + q.shape[2:]), 1, 0)
    out = jnp.moveaxis(lax.map(fn, qb), 0, 1)
    return out.reshape((b, n) + out.shape[3:])


def sink_attention(q, k, v, sink):
    b, m = q.shape[:2]
    qg = q.reshape(b, m, A_KV_HEADS, A_GROUP, HEAD_DIM)
    s = jnp.einsum('bqkgd,bskd->bkgqs', qg, k, preferred_element_type=jnp.float32) * (HEAD_DIM ** -0.5)
    snk = jnp.broadcast_to(sink.astype(jnp.float32).reshape(1, A_KV_HEADS, A_GROUP, 1, 1), s.shape[:-1] + (1,))
    p = jax.nn.softmax(jnp.concatenate([s, snk], axis=-1), axis=-1)[..., :-1]
    o = jnp.einsum('bkgqs,bskd->bqkgd', p.astype(v.dtype), v)
    return o.reshape(b, m, A_HEADS, HEAD_DIM)


def banded_window_attention(q, k, v, k_ctx, v_ctx, sink):
    b, n = q.shape[:2]
    nb = n // BLOCK
    n_ctx = k_ctx.shape[1]
    scale = HEAD_DIM ** -0.5
    qb = q.reshape(b, nb, BLOCK, A_KV_HEADS, A_GROUP, HEAD_DIM)
    pad = ((0, 0), (BLOCK, BLOCK), (0, 0), (0, 0))
    idx = jnp.arange(nb)[:, None] * BLOCK + jnp.arange(3 * BLOCK)[None, :]
    kw = jnp.pad(k, pad)[:, idx]
    vw = jnp.pad(v, pad)[:, idx]
    qpos = idx[:, BLOCK:2 * BLOCK]
    kpos = idx[:, None, :]
    mask = (jnp.abs(qpos[:, :, None] - kpos) <= A_WINDOW) & (kpos >= BLOCK) & (kpos < n + BLOCK)
    s_lat = jnp.einsum('bnqkgd,bnskd->bnkgqs', qb, kw, preferred_element_type=jnp.float32) * scale
    s_lat = jnp.where(mask[None, :, None, None], s_lat, -jnp.inf)
    s_ctx = jnp.einsum('bnqkgd,bckd->bnkgqc', qb, k_ctx, preferred_element_type=jnp.float32) * scale
    snk = jnp.broadcast_to(sink.astype(jnp.float32).reshape(1, 1, A_KV_HEADS, A_GROUP, 1, 1), s_lat.shape[:-1] + (1,))
    p = jax.nn.softmax(jnp.concatenate([s_lat, s_ctx, snk], axis=-1), axis=-1)
    w = 3 * BLOCK
    o = (jnp.einsum('bnkgqs,bnskd->bnqkgd', p[..., :w].astype(v.dtype), vw)
         + jnp.einsum('bnkgqc,bckd->bnqkgd', p[..., w:w + n_ctx].astype(v.dtype), v_ctx))
    return o.reshape(b, n, A_HEADS, HEAD_DIM)


def diff_attention(q, keys, vals, lam):
    s = jnp.concatenate([jnp.einsum('bqhjd,bshjd->bhjqs', q, k, preferred_element_type=jnp.float32)
                         for k in keys], axis=-1) * (B_HALF ** -0.5)
    p = jax.nn.softmax(s, axis=-1)
    w = p[:, :, 0] - lam * p[:, :, 1]
    o, off = 0, 0
    for v in vals:
        sl = v.shape[1]
        o = o + jnp.einsum('bhqs,bshd->bqhd', w[..., off:off + sl].astype(v.dtype), v)
        off += sl
    return o


def depthwise_conv_centred(x, w, bias):
    y = lax.conv_general_dilated(x, w[:, None, :].astype(x.dtype), window_strides=(1,),
                                 padding=[(C_CONV // 2, C_CONV // 2)],
                                 dimension_numbers=('NWC', 'WIO', 'NWC'),
                                 feature_group_count=x.shape[-1])
    return y + bias.astype(x.dtype)


def ssd_chunked_scan(x, dt, a, bm, cm, h0):
    b, t = x.shape[:2]
    nc = t // C_CHUNK
    x = x.reshape(b, nc, C_CHUNK, C_GROUPS, C_HPG, C_HEAD_DIM)
    dt = dt.reshape(b, nc, C_CHUNK, C_GROUPS, C_HPG)
    bm = bm.reshape(b, nc, C_CHUNK, C_GROUPS, C_STATE)
    cm = cm.reshape(b, nc, C_CHUNK, C_GROUPS, C_STATE)
    cum = jnp.cumsum(dt * a.reshape(C_GROUPS, C_HPG), axis=2)
    seg = cum[:, :, :, None] - cum[:, :, None, :]
    tri = jnp.tril(jnp.ones((C_CHUNK, C_CHUNK), bool))[:, :, None, None]
    decay = jnp.exp(jnp.where(tri, seg, -jnp.inf))
    cb = jnp.einsum('bclgn,bcsgn->bclsg', cm, bm)
    wts = cb[..., None] * decay * dt[:, :, None]
    y_diag = jnp.einsum('bclsgh,bcsghp->bclghp', wts, x)
    to_end = jnp.exp(cum[:, :, -1:] - cum) * dt
    states = jnp.einsum('bclgn,bclgh,bclghp->bcghpn', bm, to_end, x)
    chunk_decay = jnp.exp(cum[:, :, -1])

    def step(h, inp):
        s_c, d_c = inp
        return d_c[..., None, None] * h + s_c, h

    h0g = h0.reshape(b, C_GROUPS, C_HPG, C_HEAD_DIM, C_STATE)
    h_last, h_prev = lax.scan(step, h0g, (jnp.moveaxis(states, 1, 0), jnp.moveaxis(chunk_decay, 1, 0)))
    h_prev = jnp.moveaxis(h_prev, 0, 1)
    y_off = jnp.einsum('bclgn,bcghpn->bclghp', cm, h_prev) * jnp.exp(cum)[..., None]
    y = (y_diag + y_off).reshape(b, t, C_HEADS, C_HEAD_DIM)
    return y, h_last.reshape(b, C_HEADS, C_HEAD_DIM, C_STATE)


def ssd_mixer(z, xbc, dt_raw, lw, h0f, h0b):
    b, n = z.shape[:2]
    f32 = jnp.float32
    xbc = jax.nn.silu(depthwise_conv_centred(xbc, lw['conv_w'], lw['conv_b'])).astype(f32)
    x = xbc[..., :C_INNER].reshape(b, n, C_HEADS, C_HEAD_DIM)
    bm = xbc[..., C_INNER:C_INNER + C_GROUPS * C_STATE].reshape(b, n, C_GROUPS, C_STATE)
    cm = xbc[..., C_INNER + C_GROUPS * C_STATE:].reshape(b, n, C_GROUPS, C_STATE)
    dt = jax.nn.softplus(dt_raw.astype(f32).reshape(b, n, 2, C_HEADS) + lw['dt_bias'].astype(f32))
    a = -jnp.exp(lw['a_log'].astype(f32))
    rev = lambda t: jnp.flip(t, axis=1)
    y_f, h_f = ssd_chunked_scan(x, dt[:, :, 0], a[0], bm, cm, h0f.astype(f32))
    y_b, h_b = ssd_chunked_scan(rev(x), rev(dt[:, :, 1]), a[1], rev(bm), rev(cm), h0b.astype(f32))
    y = y_f + rev(y_b) + lw['d_skip'].astype(f32)[:, None] * x
    y = y.reshape(b, n, C_INNER) * jax.nn.silu(z.astype(f32))
    yg = y.reshape(b, n, C_GROUPS, C_INNER // C_GROUPS)
    yg = yg * lax.rsqrt(jnp.mean(yg * yg, axis=-1, keepdims=True) + EPS)
    y = yg.reshape(b, n, C_INNER) * lw['ssm_norm'].astype(f32)
    return y.astype(z.dtype), h_f, h_b


def lambda_init(layer):
    return 0.8 - 0.6 * math.exp(-0.3 * layer)


def trunk_layer(h, cond, lw, lam_init, ctx_cache):
    b, n = h.shape[:2]
    sh1, sc1, g1, sh2, sc2, g2 = adaln(cond, lw['w_ada'], lw['b_ada'])
    u = rmsnorm(h, lw['norm_mix']) * (1 + sc1) + sh1
    qa, ka, va, qd, kd, vd, z, xbc, dt_raw = split_projection(u @ lw['w_in'])
    qa = qa.reshape(b, n, A_HEADS, HEAD_DIM)
    ka = ka.reshape(b, n, A_KV_HEADS, HEAD_DIM)
    va = va.reshape(b, n, A_KV_HEADS, HEAD_DIM)
    qd = qd.reshape(b, n, B_HEADS, 2, B_HALF)
    kd = kd.reshape(b, n, B_HEADS, 2, B_HALF)
    vd = vd.reshape(b, n, B_HEADS, HEAD_DIM)
    lv = lw['diff_lambda'].astype(jnp.float32)
    lam = jnp.exp(jnp.sum(lv[0] * lv[1])) - jnp.exp(jnp.sum(lv[2] * lv[3])) + lam_init
    sink = lw['attn_sink']
    if ctx_cache is None:
        oa = sweep_query_blocks(lambda qi: sink_attention(qi, ka, va, sink), qa)
        od = sweep_query_blocks(lambda qi: diff_attention(qi, (kd,), (vd,), lam), qd)
        h0f = jnp.zeros((b, C_HEADS, C_HEAD_DIM, C_STATE), jnp.float32)
        h0b = h0f
    else:
        ck_a, cv_a, ck_d, cv_d, h0f, h0b = ctx_cache
        cos_a, sin_a = axial_rope_tables(n, HEAD_DIM)
        cos_d, sin_d = axial_rope_tables(n, B_HALF)
        qa_r = apply_rope(qa, cos_a, sin_a)
        ka_r = apply_rope(ka, cos_a, sin_a)
        qd_r = apply_rope(qd, cos_d, sin_d)
        kd_r = apply_rope(kd, cos_d, sin_d)
        oa = banded_window_attention(qa_r, ka_r, va, ck_a, cv_a, sink)
        ck_d = ck_d.reshape(b, ck_d.shape[1], B_HEADS, 2, B_HALF)
        od = sweep_query_blocks(lambda qi: diff_attention(qi, (kd_r, ck_d), (vd, cv_d), lam), qd_r)
    od = rmsnorm(od, lw['diff_norm']) * (1.0 - lam_init)
    oc, hf, hb = ssd_mixer(z, xbc, dt_raw, lw, h0f, h0b)
    mixed = jnp.concatenate([oa.reshape(b, n, A_Q), od.reshape(b, n, B_QKV), oc], axis=-1) @ lw['w_out']
    h = h + g1 * mixed
    u = rmsnorm(h, lw['norm_ffn']) * (1 + sc2) + sh2
    gate, up = jnp.split(u @ lw['w_gate_up'], 2, axis=-1)
    h = h + g2 * ((jax.nn.silu(gate) * up) @ lw['w_down'])
    if ctx_cache is None:
        return h, (ka, va, kd.reshape(b, n, B_HEADS, HEAD_DIM), vd, hf.astype(h.dtype), hb.astype(h.dtype))
    return h, None


def setup_inputs(seed: int = 0) -> dict:
    key = jax.random.key(seed)
    ks = jax.random.split(key, 32)
    f32 = jnp.float32
    nrm = lambda k, shape, scale: jax.random.normal(k, shape, f32) * scale
    dt0 = jnp.exp(jax.random.uniform(ks[20], (DEPTH, 2, C_HEADS), f32) * (math.log(0.1) - math.log(0.001)) + math.log(0.001))
    return {
        'x_prompt': nrm(ks[0], (BATCH, SEQ, D_MODEL), 1.0),
        'x_sample': nrm(ks[1], (DEC_BATCH, DEC_SEQ, D_MODEL), 1.0),
        'cache_attn_k': nrm(ks[2], (DEC_BATCH, DEPTH, PAST_LEN, A_KV_HEADS, HEAD_DIM), 1.0),
        'cache_attn_v': nrm(ks[3], (DEC_BATCH, DEPTH, PAST_LEN, A_KV_HEADS, HEAD_DIM), 1.0),
        'cache_diff_k': nrm(ks[4], (DEC_BATCH, DEPTH, PAST_LEN, B_HEADS, HEAD_DIM), 1.0),
        'cache_diff_v': nrm(ks[5], (DEC_BATCH, DEPTH, PAST_LEN, B_HEADS, HEAD_DIM), 1.0),
        'state_ssm_fwd': nrm(ks[6], (DEC_BATCH, DEPTH, C_HEADS, C_HEAD_DIM, C_STATE), 0.1),
        'state_ssm_bwd': nrm(ks[7], (DEC_BATCH, DEPTH, C_HEADS, C_HEAD_DIM, C_STATE), 0.1),
        'c': nrm(ks[8], (DEC_BATCH, D_MODEL), 1.0),
        'c_ctx': nrm(ks[9], (D_MODEL,), 1.0),
        'w_ada': nrm(ks[10], (DEPTH, D_MODEL, N_MOD * D_MODEL), 0.5 * D_MODEL ** -0.5),
        'b_ada': nrm(ks[11], (DEPTH, N_MOD * D_MODEL), 0.01),
        'norm_mix': 1.0 + nrm(ks[12], (DEPTH, D_MODEL), 0.01),
        'norm_ffn': 1.0 + nrm(ks[13], (DEPTH, D_MODEL), 0.01),
        'w_in': nrm(ks[14], (DEPTH, D_MODEL, IN_COLS), D_MODEL ** -0.5),
        'attn_sink': nrm(ks[15], (DEPTH, A_HEADS), 0.5),
        'diff_lambda': nrm(ks[16], (DEPTH, 4, B_HALF), 0.1),
        'diff_norm': 1.0 + nrm(ks[17], (DEPTH, HEAD_DIM), 0.01),
        'conv_w': nrm(ks[18], (DEPTH, C_CONV, C_CONV_CH), C_CONV ** -0.5),
        'conv_b': nrm(ks[19], (DEPTH, C_CONV_CH), 0.01),
        'dt_bias': dt0 + jnp.log(-jnp.expm1(-dt0)),
        'a_log': jnp.log(jax.random.uniform(ks[21], (DEPTH, 2, C_HEADS), f32, 1.0, 16.0)),
        'd_skip': 1.0 + nrm(ks[22], (DEPTH, C_HEADS), 0.01),
        'ssm_norm': 1.0 + nrm(ks[23], (DEPTH, C_INNER), 0.01),
        'w_out': nrm(ks[24], (DEPTH, MIX_W, D_MODEL), MIX_W ** -0.5),
        'w_gate_up': nrm(ks[25], (DEPTH, D_MODEL, 2 * FF), D_MODEL ** -0.5),
        'w_down': nrm(ks[26], (DEPTH, FF, D_MODEL), FF ** -0.5),
        'norm_final': 1.0 + nrm(ks[27], (D_MODEL,), 0.01),
    }


def reference(x_prompt, x_sample, cache_attn_k, cache_attn_v, cache_diff_k, cache_diff_v,
              state_ssm_fwd, state_ssm_bwd, c, c_ctx, w_ada, b_ada, norm_mix, norm_ffn, w_in,
              attn_sink, diff_lambda, diff_norm, conv_w, conv_b, dt_bias, a_log, d_skip,
              ssm_norm, w_out, w_gate_up, w_down, norm_final):
    def layer_weights(l):
        return {'w_ada': w_ada[l], 'b_ada': b_ada[l], 'norm_mix': norm_mix[l], 'norm_ffn': norm_ffn[l],
                'w_in': w_in[l], 'attn_sink': attn_sink[l], 'diff_lambda': diff_lambda[l],
                'diff_norm': diff_norm[l], 'conv_w': conv_w[l], 'conv_b': conv_b[l],
                'dt_bias': dt_bias[l], 'a_log': a_log[l], 'd_skip': d_skip[l], 'ssm_norm': ssm_norm[l],
                'w_out': w_out[l], 'w_gate_up': w_gate_up[l], 'w_down': w_down[l]}

    h = x_prompt
    ctx_layers = []
    for l in range(DEPTH):
        h, ctx = trunk_layer(h, c_ctx, layer_weights(l), lambda_init(l), None)
        ctx_layers.append(ctx)
    y_prompt = rmsnorm(h, norm_final)
    new_attn_k = jnp.stack([t[0] for t in ctx_layers], axis=1)
    new_attn_v = jnp.stack([t[1] for t in ctx_layers], axis=1)
    new_diff_k = jnp.stack([t[2] for t in ctx_layers], axis=1)
    new_diff_v = jnp.stack([t[3] for t in ctx_layers], axis=1)
    new_ssm_fwd = jnp.stack([t[4] for t in ctx_layers], axis=1)
    new_ssm_bwd = jnp.stack([t[5] for t in ctx_layers], axis=1)

    h = x_sample
    for l in range(DEPTH):
        cache = (cache_attn_k[:, l], cache_attn_v[:, l], cache_diff_k[:, l], cache_diff_v[:, l],
                 state_ssm_fwd[:, l], state_ssm_bwd[:, l])
        h, _ = trunk_layer(h, c, layer_weights(l), lambda_init(l), cache)
    y_sample = rmsnorm(h, norm_final)
    return (y_prompt, y_sample, new_attn_k, new_attn_v, new_diff_k, new_diff_v, new_ssm_fwd, new_ssm_bwd)
```

```python
import math
from contextlib import ExitStack
import numpy as np
import ml_dtypes
import concourse.bass as bass
import concourse.mybir as mybir
from concourse.bass_utils import run_bass_kernel_spmd

F32 = mybir.dt.float32
BF16 = mybir.dt.bfloat16
AF = mybir.ActivationFunctionType
ALU = mybir.AluOpType
AX = mybir.AxisListType

DEPTH = 4
EPS = 1e-6
NBL = 95
PFL = 209
PRL = 340
NEG = -30000.0


class _Op:
    __slots__ = ("stream", "is_dma", "fn", "deps", "signals", "sig_idx", "sem", "target", "alldma")

    def __init__(s, stream, is_dma, fn):
        s.stream = stream
        s.is_dma = is_dma
        s.fn = fn
        s.deps = ()
        s.signals = False
        s.sig_idx = 0
        s.sem = None
        s.target = 0
        s.alldma = False


class Emitter:
    STREAMS = ("pe", "act", "dve", "pool", "sp")

    def __init__(s, same_sync=True, n_dma_sems=40):
        s.ops = []
        s.res = {}
        s.same_sync = same_sync
        s.n_dma_sems = n_dma_sems
        s.out_dmas = []
        s.last = {}

    def _track(s, op, r, w):
        deps = set()
        for x in r:
            st = s.res.get(x)
            if st is None:
                st = s.res[x] = [None, {}]
            if st[0] is not None:
                deps.add(st[0])
        for x in w:
            st = s.res.get(x)
            if st is None:
                st = s.res[x] = [None, {}]
            if st[0] is not None:
                deps.add(st[0])
            deps.update(st[1].values())
        for x in r:
            key = id(op) if op.is_dma else op.stream
            s.res[x][1][key] = op
        for x in w:
            st = s.res[x]
            st[0] = op
            st[1] = {}
        deps.discard(op)
        op.deps = deps

    def op(s, stream, fn, r=(), w=()):
        if stream != "pe":
            psr = [x for x in r if isinstance(x, tuple) and x[0] == "ps"]
            if psr:
                w = list(w) + [x for x in psr if x not in w]
        o = _Op(stream, False, fn)
        s._track(o, r, w)
        s.ops.append(o)
        s.last[stream] = o
        return o

    def dma(s, queue, out, in_, r=(), w=(), final=False, **kw):
        import os as _os
        if final and _os.environ.get("FINQ"):
            queue = _os.environ["FINQ"]
        if queue == "act" and _os.environ.get("STQ"):
            queue = _os.environ["STQ"]
        def fn(eng, out=out, in_=in_, kw=kw):
            return eng.dma_start(out=out, in_=in_, **kw)
        o = _Op(queue, True, fn)
        s._track(o, r, w)
        s.ops.append(o)
        if final:
            s.out_dmas.append(o)
        return o

    def barrier(s):
        lasts = set(v for k, v in s.last.items())
        for st in s.STREAMS:
            o = _Op(st, False, None)
            o.deps = set(lasts)
            o.alldma = True
            s.ops.append(o)
        s.res = {}

    def emit(s, nc, stack):
        esem = {}
        for st in ("pe", "act", "dve", "pool"):
            esem[st] = stack.enter_context(nc.semaphore("es_" + st))
        dsems = [stack.enter_context(nc.semaphore("ds%d" % i)) for i in range(s.n_dma_sems)]
        dcount = [0] * s.n_dma_sems
        dnext = 0
        fin = _Op("sp", False, None)
        fin.deps = set(s.out_dmas)
        fin.alldma = True
        s.ops.append(fin)

        def skip(o, d):
            return d.stream == o.stream and not o.is_dma and (o.stream == "pe" or not s.same_sync)

        for o in s.ops:
            for d in o.deps:
                if d.is_dma or skip(o, d):
                    continue
                d.signals = True
        lists = {st: [] for st in s.STREAMS}
        known = {st: {} for st in s.STREAMS}
        cnt = {st: 0 for st in s.STREAMS}
        for o in s.ops:
            items = lists[o.stream]
            kn = known[o.stream]
            waits = []
            for d in o.deps:
                if d.is_dma:
                    waits.append((d.sem, d.target))
                elif not skip(o, d):
                    waits.append((esem[d.stream], d.sig_idx))
            if o.alldma:
                for k in range(s.n_dma_sems):
                    if dcount[k] > 0:
                        waits.append((dsems[k], dcount[k]))
            inc = None
            if o.is_dma:
                k = dnext
                dnext = (dnext + 1) % s.n_dma_sems
                if dcount[k] > 0:
                    waits.append((dsems[k], dcount[k]))
                dcount[k] += 16
                o.sem = dsems[k]
                o.target = dcount[k]
                inc = (o.sem, 16)
            elif o.signals:
                cnt[o.stream] += 1
                o.sig_idx = cnt[o.stream]
                inc = (esem[o.stream], 1)
            for sem, val in waits:
                key = id(sem)
                if kn.get(key, 0) < val:
                    kn[key] = val
                    items.append(("w", sem, val))
            if o.fn is not None:
                items.append(("o", o.fn, inc))
        s.stats = {st: len(lists[st]) for st in s.STREAMS}

        def run(eng, items):
            for it in items:
                if it[0] == "w":
                    eng.wait_ge(it[1], it[2])
                else:
                    ins = it[1](eng)
                    if it[2] is not None:
                        ins.then_inc(it[2][0], it[2][1])

        with nc.Block() as block:
            @block.tensor
            def _(e):
                run(e, lists["pe"])

            @block.scalar
            def _(e):
                run(e, lists["act"])

            @block.vector
            def _(e):
                run(e, lists["dve"])

            @block.gpsimd
            def _(e):
                run(e, lists["pool"])

            @block.sync
            def _(e):
                run(e, lists["sp"])


def lambda_init(layer):
    return 0.8 - 0.6 * math.exp(-0.3 * layer)


class Arena:
    def __init__(s, t, n, name):
        s.t, s.n, s.off, s.name, s.gen = t, n, 0, name, 0

    def reset(s):
        s.off = 0
        s.gen += 1

    def take(s, n):
        assert s.off + n <= s.n, (s.name, s.off, n, s.n)
        ap = s.t[:, s.off:s.off + n]
        res = (s.name, s.gen, s.off)
        s.off += n
        return ap, res


class Ring:
    def __init__(s, arena, k, n):
        s.items = [arena.take(n) for _ in range(k)]
        s.i = 0

    def next(s):
        it = s.items[s.i % len(s.items)]
        s.i += 1
        return it


def build(depth=DEPTH, do_sample=True, stages=('s1', 'ctx', 'attn', 'ssd', 's3'), pro=(1, 1)):
    nc = bass.Bass("TRN2", target_bir_lowering=False)
    em = Emitter(same_sync=True)

    def D(name, shape, dt=F32, kind="ExternalInput"):
        return nc.dram_tensor(name, list(shape), dt, kind=kind).ap()

    TS = 4096
    xpT = D("xpT", [16, 128, 512])
    xsT = D("xsT", [16, 128, TS])
    cakT = D("cakT", [DEPTH, 2, 128, 256])
    cav = D("cav", [DEPTH, 256, 256])
    cdkT = D("cdkT", [DEPTH, 4, 128, 256])
    cdv = D("cdv", [DEPTH, 256, 512])
    s0f = D("s0f", [DEPTH, 128, 1024])
    s0b = D("s0b", [DEPTH, 128, 1024])
    wf = D("wf", [DEPTH * NBL, 128, 4096])
    wada = D("wada", [DEPTH * 48, 128, 4096])
    pfm = D("pfm", [128, DEPTH * PFL + 16])
    prep = D("prep", [128, DEPTH * PRL])
    condfm = D("condfm", [128, 32])
    cst = D("cst", [128, 896])
    rope = D("rope", [4, 128, TS])
    ypT = D("ypT", [16, 128, 512], kind="ExternalOutput")
    ysT = D("ysT", [16, 128, TS], kind="ExternalOutput")
    nak = D("nak", [DEPTH, 512, 256], kind="ExternalOutput")
    nav = D("nav", [DEPTH, 512, 256], kind="ExternalOutput")
    ndk = D("ndk", [DEPTH, 512, 512], kind="ExternalOutput")
    ndv = D("ndv", [DEPTH, 512, 512], kind="ExternalOutput")
    nsf = D("nsf", [2, DEPTH, 128, 1024], kind="ExternalOutput")
    nsb = D("nsb", [2, DEPTH, 128, 1024], kind="ExternalOutput")
    wbfL = [D("wbf%d" % i, [NBL, 128, 4096], BF16, kind="Internal") for i in range(DEPTH)]

    class _W:
        def __getitem__(s, b):
            return wbfL[b // NBL][b % NBL]
    wbf = _W()

    def scratch(pfx, T, TK):
        return dict(
            hT=D(pfx + "hT", [16, 128, T], F32, "Internal"),
            qaT=D(pfx + "qaT", [4, 128, T], BF16, "Internal"),
            kaT=D(pfx + "kaT", [2, 128, TK], BF16, "Internal"),
            va=D(pfx + "va", [TK, 256], BF16, "Internal"),
            qdT=D(pfx + "qdT", [4, 128, T], BF16, "Internal"),
            kdT=D(pfx + "kdT", [4, 128, TK], BF16, "Internal"),
            vd=D(pfx + "vd", [TK, 512], BF16, "Internal"),
            zs=D(pfx + "zs", [T, 1024], F32, "Internal"),
            xcT=D(pfx + "xcT", [12, 128, T], F32, "Internal"),
            dts=D(pfx + "dts", [T, 32], F32, "Internal"),
            mixT=D(pfx + "mixT", [16, 128, T], BF16, "Internal"),
            xtm=D(pfx + "xtm", [2, T, 512], BF16, "Internal"),
            btm=D(pfx + "btm", [2, T, 128], BF16, "Internal"),
            bT=D(pfx + "bT", [2, 128, T], BF16, "Internal"),
            cT=D(pfx + "cT", [2, 128, T], BF16, "Internal"),
            hbp=D(pfx + "hbp", [2, T // 128, 128, 512], BF16, "Internal"),
        )

    paths = [dict(name="P", T=512, nseq=2, L=256, sample=False, xT=xpT, yT=ypT, cond=0, **scratch("P", 512, 512))]
    if do_sample:
        paths.append(dict(name="S", T=TS, nseq=1, L=TS, sample=True, xT=xsT, yT=ysT, cond=1, **scratch("S", TS, TS + 256)))

    with ExitStack() as st:
        def sb(name, shape, dt):
            return st.enter_context(nc.sbuf_tensor(name, shape, dt))

        A32t = sb("A32", [128, 20480], F32)
        A16t = sb("A16", [128, 36864], BF16)
        A32 = Arena(A32t, 20480, "A32")
        A16 = Arena(A16t, 36864, "A16")
        wr = sb("wr", [128, 4, 4096], BF16)
        cst32 = sb("cst32", [128, 896], F32)
        cstb = sb("cstb", [128, 1024], BF16)
        pf = sb("pf", [128, DEPTH * PFL + 16], F32)
        pr = sb("pr", [128, DEPTH * PRL], F32)
        mods = sb("mods", [128, DEPTH * 96 * 2], F32)
        a12 = sb("a12", [128, DEPTH * 2 * 16 * 2], F32)
        sm = sb("sm", [128, 512], F32)
        ps = st.enter_context(nc.psum_tensor("ps", [128, 8, 512], F32))
        psi = [0]
        pti = [0]
        wri = [0]

        pmode = [6]

        def psn():
            k = psi[0] % pmode[0]
            psi[0] += 1
            return ps[:, k, :], ("ps", k)

        def psl(k):
            return ps[:, k, :], ("ps", k)

        def ptn():
            k = 6 + pti[0] % 2
            pti[0] += 1
            return ps[:, k, :].bitcast(BF16)[:, 0:512], ("ps", k)

        def mm(out, lhsT, rhs, start, stop, r, w):
            em.op("pe", lambda e: e.matmul(out, lhsT=lhsT, rhs=rhs, start=start, stop=stop), r=r, w=w)

        def load_w(bidx):
            k = wri[0] % 4
            wri[0] += 1
            em.dma("sp", wr[:, k, :], wbf[bidx], r=[("wbf", bidx)], w=[("wr", k)])
            return wr[:, k, :], ("wr", k)

        cpi = [0]

        def copy(out, in_, r, w, eng=None):
            import os as _os
            if eng is None and _os.environ.get("CPENG"):
                eng = _os.environ["CPENG"]
            if eng is None:
                eng = ("act", "dve")[cpi[0] % 2]
                cpi[0] += 1
            if eng == "act":
                em.op("act", lambda e: e.copy(out, in_), r=r, w=w)
            else:
                em.op(eng, lambda e: e.tensor_copy(out, in_), r=r, w=w)

        em.dma("sp", cst32[:], cst, w=["cst32"])
        em.dma("sp", pf[:], pfm, w=["pf"])
        em.dma("sp", pr[:], prep, w=["pr"])
        em.op("dve", lambda e: e.tensor_copy(cstb[:, 0:896], cst32[:]), r=["cst32"], w=["cstb"])
        em.op("dve", lambda e: e.memset(cstb[:, 896:1024], 1.0), w=["cstb"])
        CB = ["cstb"]
        ident_b = cstb[:, 0:128]
        tri_b = cstb[:, 128:256]
        triR_b = cstb[:, 256:384]
        mF_b = cstb[:, 384:512]
        mB_b = cstb[:, 512:640]
        pswA_b = cstb[:, 640:768]
        pswB_b = cstb[:, 768:896]
        ones_b = cstb[:, 896:1024]
        ones32 = sm[:, 0:128]
        epsT = sm[:, 128:129]
        em.op("dve", lambda e: e.memset(ones32, 1.0), w=["sm_c"])
        em.op("dve", lambda e: e.memset(epsT, EPS), w=["sm_c"])
        SMC = ["sm_c"]
        def pfc(l, off, n=1):
            return pf[:, l * PFL + off: l * PFL + off + n]

        def prc(l, off, n=1):
            return pr[:, l * PRL + off: l * PRL + off + n]

        def esink(l):
            return sm[:, 130 + 8 * l: 134 + 8 * l]

        def lam_ap(l):
            return sm[:, 134 + 8 * l: 135 + 8 * l]

        def nlam_ap(l):
            return sm[:, 135 + 8 * l: 136 + 8 * l]

        def dnl_ap(l):
            return sm[:, 136 + 8 * l: 137 + 8 * l]

        def aneg(l):
            return sm[:, 200 + 32 * l: 232 + 32 * l]

        tmpa = sm[:, 340:404]
        tmpb = sm[:, 404:408]
        for l in range(DEPTH):
            em.op("act", lambda e, l=l: e.activation(esink(l), prc(l, 0, 4), AF.Exp), r=["pr"], w=["sm_p"])
            em.op("act", lambda e, l=l: e.activation(aneg(l), prc(l, 292, 32), AF.Exp), r=["pr"], w=["sm_p"])
            em.op("dve", lambda e, l=l: e.tensor_scalar(aneg(l), aneg(l), -1.0, None, ALU.mult), r=["sm_p"], w=["sm_p"])
            for j in range(2):
                em.op("dve", lambda e, l=l, j=j: e.tensor_tensor(out=tmpa, in0=prc(l, 4 + 128 * j, 64), in1=prc(l, 68 + 128 * j, 64), op=ALU.mult), r=["pr", "sm_p"], w=["sm_t"])
                em.op("dve", lambda e, j=j: e.reduce_sum(tmpb[:, j:j + 1], tmpa, axis=AX.X), r=["sm_t"], w=["sm_t2"])
            em.op("act", lambda e: e.activation(tmpb[:, 2:4], tmpb[:, 0:2], AF.Exp), r=["sm_t2"], w=["sm_t3"])
            em.op("dve", lambda e, l=l: e.tensor_tensor(out=lam_ap(l), in0=tmpb[:, 2:3], in1=tmpb[:, 3:4], op=ALU.subtract), r=["sm_t3"], w=["sm_p"])
            em.op("dve", lambda e, l=l: e.tensor_scalar(lam_ap(l), lam_ap(l), float(lambda_init(l)), None, ALU.add), r=["sm_p"], w=["sm_p"])
            em.op("dve", lambda e, l=l: e.tensor_scalar(nlam_ap(l), lam_ap(l), -1.0, None, ALU.mult), r=["sm_p"], w=["sm_p"])
            em.op("dve", lambda e, l=l: e.tensor_scalar(dnl_ap(l), pfc(l, 208), float(1.0 - lambda_init(l)), None, ALU.mult), r=["pf", "sm_p"], w=["sm_p"])
        SMP = ["sm_p"]

        A32.reset(); A16.reset()
        stg = [A32.take(4096) for _ in range(3)]
        cbf = [A16.take(4096) for _ in range(3)]
        nb_used = depth * NBL if pro[0] else 0
        for b in range(nb_used):
            k = b % 3
            em.dma("sp", stg[k][0], wf[b], w=[stg[k][1]])
            copy(cbf[k][0], stg[k][0], r=[stg[k][1]], w=[cbf[k][1]], eng=("pool", "dve", "act")[b % 3])
            em.dma("act", wbf[b], cbf[k][0], r=[cbf[k][1]], w=[("wbf", b)])
        cnd, cndr = A32.take(32)
        scn, scnr = A32.take(32)
        em.dma("sp", cnd, condfm, w=[cndr])
        em.op("act", lambda e: e.activation(scn, cnd, AF.Silu), r=[cndr], w=[scnr])
        scv = scn.rearrange("p (k c) -> p k c", c=2)
        modv = mods[:].rearrange("p (l m c) -> p l m c", l=DEPTH, c=2)
        for l in range(depth if pro[1] else 0):
            for j in range(48):
                k = (l * 48 + j) % 3
                em.dma("sp", stg[k][0], wada[l * 48 + j], w=[stg[k][1]])
                wv = stg[k][0].rearrange("p (k c) -> p k c", k=16)
                bank, br = psn()
                for cc in range(2):
                    for kk in range(16):
                        mm(bank[:, cc * 2:cc * 2 + 2], wv[:, kk, cc * 128:(cc + 1) * 128], scv[:, kk, :], kk == 0, kk == 15, [stg[k][1], scnr], [br])
                bias = pfc(l, 32 + 2 * j, 2).unsqueeze(2).to_broadcast([128, 2, 2])
                em.op("dve", lambda e, l=l, j=j, bank=bank, bias=bias: e.tensor_tensor(out=modv[:, l, 2 * j:2 * j + 2, :], in0=bank[:, 0:4].rearrange("p (m c) -> p m c", c=2), in1=bias, op=ALU.add), r=[br, "pf"], w=["mods"])
        a12v = a12[:].rearrange("p (l j k c) -> p l j k c", l=DEPTH, j=2, c=2)
        for l in range(depth):
            for j in range(2):
                sc = modv[:, l, 16 + 48 * j:32 + 48 * j, :]
                gn = pfc(l, 16 * j, 16).unsqueeze(2).to_broadcast([128, 16, 2])
                em.op("dve", lambda e, l=l, j=j, sc=sc: e.tensor_scalar(a12v[:, l, j], sc, 1.0, None, ALU.add), r=["mods"], w=["a12"])
                em.op("dve", lambda e, l=l, j=j, gn=gn: e.tensor_tensor(out=a12v[:, l, j], in0=a12v[:, l, j], in1=gn, op=ALU.mult), r=["a12", "pf"], w=["a12"])

        def modc(l, m, c, cond):
            return modv[:, l, m * 16 + c, cond:cond + 1]

        def norm_group(P, h32, h32r, uT, uTr, a_of, sh_of, tmp32, sqr, rs, rsr):
            bank, br = psn()
            for c in range(16):
                sq, sqres = sqr.next()
                em.op("act", lambda e, sq=sq, c=c: e.activation(sq, h32[:, c, :], AF.Square), r=[h32r], w=[sqres])
                mm(bank, ones_b, sq, c == 0, c == 15, [sqres] + CB, [br])
            em.op("act", lambda e: e.activation(rs, bank, AF.Sqrt, bias=epsT, scale=1.0 / 2048), r=[br] + SMC, w=[rsr])
            em.op("dve", lambda e: e.reciprocal(rs, rs), r=[rsr], w=[rsr])
            for c in range(16):
                t, tr = tmp32.next()
                em.op("dve", lambda e, t=t, c=c: e.tensor_tensor(out=t, in0=h32[:, c, :], in1=rs, op=ALU.mult), r=[h32r, rsr], w=[tr])
                if uT is not None:
                    em.op("act", lambda e, t=t, c=c: e.activation(uT[:, c, :], t, AF.Identity, bias=sh_of(c), scale=a_of(c)), r=[tr, "mods", "a12"], w=[uTr])
                else:
                    o, orr = tmp32.next()
                    em.op("act", lambda e, t=t, o=o, c=c: e.activation(o, t, AF.Copy, scale=pf[:, DEPTH * PFL + c:DEPTH * PFL + c + 1]), r=[tr, "pf"], w=[orr])
                    em.dma("act", P["yT"][c, :, P["g0"]:P["g0"] + 512], o, r=[orr], final=True)

        def stage1(P, l, g):
            pmode[0] = 6
            cond = P["cond"]
            smp = P["sample"]
            T0 = g * 512
            A32.reset(); A16.reset()
            h32f, h32r = A32.take(8192)
            h32 = h32f.rearrange("p (c t) -> p c t", c=16)
            rs, rsr = A32.take(512)
            tmp32 = Ring(A32, 3, 512)
            o32 = Ring(A32, 4, 512)
            uTf, uTr = A16.take(8192)
            uT = uTf.rearrange("p (c t) -> p c t", c=16)
            sqr = Ring(A16, 2, 512)
            o16 = Ring(A16, 4, 512)
            src = P["xT"] if l == 0 else P["hT"]
            em.dma("sp", h32, src[:, :, T0:T0 + 512].rearrange("c p t -> p c t"), r=[(P["name"], "hT", g)], w=[h32r])
            if smp:
                rt, rtr = A32.take(2048)
                rtv = rt.rearrange("p (f t) -> p f t", f=4)
                em.dma("sp", rtv, rope[:, :, T0:T0 + 512].rearrange("f p t -> p f t"), w=[rtr])
            import os as _os
            _dbg = _os.environ.get("S1DBG", "norm,proj")
            if "norm" in _dbg:
                norm_group(P, h32, h32r, uT, uTr, lambda c: a12v[:, l, 0, c, cond:cond + 1], lambda c: modc(l, 0, c, cond), tmp32, sqr, rs, rsr)
            nm = P["name"]
            _blks = [int(x) for x in _os.environ.get("S1BLK", ",".join(str(i) for i in range(21))).split(",") if x != ""]
            for i in (_blks if "proj" in _dbg else []):
                wb, wres = load_w(l * NBL + i)
                wv = wb.rearrange("p (k c) -> p k c", k=16)
                fm = i in (0, 1, 2, 4, 5, 6, 7, 14, 15, 16, 17, 18, 19)
                tm = i in (3, 8, 9, 10, 11, 12, 13, 20) or ((not smp) and i in (2, 6, 7))
                if fm:
                    for hb in range(2):
                        bank, br = psn()
                        for k in range(16):
                            mm(bank, wv[:, k, hb * 128:(hb + 1) * 128], uT[:, k, :], k == 0, k == 15, [wres, uTr], [br])
                        if i >= 14:
                            o, orr = o32.next()
                            copy(o, bank, [br], [orr])
                            blk = 2 * (i - 14) + hb
                            em.dma("act", P["xcT"][blk, :, T0:T0 + 512], o, r=[orr], w=[(nm, "xcT", g)])
                            continue
                        if i < 2:
                            dst, dn_ = P["qaT"][2 * i + hb, :, T0:T0 + 512], "qaT"
                        elif i == 2:
                            dst, dn_ = P["kaT"][hb, :, T0:T0 + 512], "kaT"
                        elif i < 6:
                            dst, dn_ = P["qdT"][2 * (i - 4) + hb, :, T0:T0 + 512], "qdT"
                        else:
                            dst, dn_ = P["kdT"][2 * (i - 6) + hb, :, T0:T0 + 512], "kdT"
                        o, orr = o16.next()
                        if not smp:
                            copy(o, bank, [br], [orr])
                        else:
                            isA = i <= 2
                            xb, xbr = o16.next()
                            copy(xb, bank, [br], [xbr], eng="dve")
                            b2, b2r = psn()
                            mm(b2, pswA_b if isA else pswB_b, xb, True, True, [xbr] + CB, [b2r])
                            t1, t1r = tmp32.next()
                            t2, t2r = tmp32.next()
                            fc = 0 if isA else 2
                            em.op("dve", lambda e, t1=t1, bank=bank, fc=fc: e.tensor_tensor(out=t1, in0=bank, in1=rtv[:, fc, :], op=ALU.mult), r=[br, rtr], w=[t1r])
                            em.op("dve", lambda e, t2=t2, b2=b2, fc=fc: e.tensor_tensor(out=t2, in0=b2, in1=rtv[:, fc + 1, :], op=ALU.mult), r=[b2r, rtr], w=[t2r])
                            em.op("pool", lambda e, o=o, t1=t1, t2=t2: e.tensor_tensor(out=o, in0=t1, in1=t2, op=ALU.add), r=[t1r, t2r], w=[orr])
                        em.dma("act", dst, o, r=[orr], w=[(nm, dn_, g)])
                if tm:
                    ncol = 32 if i == 20 else 256
                    for tt in range(4):
                        rows = slice(T0 + tt * 128, T0 + tt * 128 + 128)
                        bank, br = psn()
                        for k in range(16):
                            mm(bank[:, 0:ncol], uT[:, k, tt * 128:(tt + 1) * 128], wv[:, k, 0:ncol], k == 0, k == 15, [wres, uTr], [br])
                        if i in (3, 8, 9) and 'nova' not in _dbg:
                            o, orr = o16.next()
                            copy(o[:, 0:256], bank[:, 0:256], [br], [orr])
                            if i == 3:
                                em.dma("act", P["va"][rows, :], o[:, 0:256], r=[orr], w=[(nm, "va", g)])
                            else:
                                em.dma("act", P["vd"][rows, (i - 8) * 256:(i - 8) * 256 + 256], o[:, 0:256], r=[orr], w=[(nm, "vd", g)])
                        if (not smp) and i in (2, 3, 6, 7, 8, 9) and 'noout' not in _dbg:
                            o, orr = o32.next()
                            copy(o[:, 0:256], bank[:, 0:256], [br], [orr])
                            if i == 2:
                                dst = nak[l, rows, :]
                            elif i == 3:
                                dst = nav[l, rows, :]
                            elif i in (6, 7):
                                dst = ndk[l, rows, (i - 6) * 256:(i - 6) * 256 + 256]
                            else:
                                dst = ndv[l, rows, (i - 8) * 256:(i - 8) * 256 + 256]
                            em.dma("act", dst, o[:, 0:256], r=[orr], final=True)
                        if 10 <= i <= 13:
                            o, orr = o32.next()
                            em.op("act", lambda e, o=o, bank=bank: e.activation(o[:, 0:256], bank[:, 0:256], AF.Silu), r=[br], w=[orr])
                            em.dma("act", P["zs"][rows, (i - 10) * 256:(i - 10) * 256 + 256], o[:, 0:256], r=[orr], w=[(nm, "zs", g)])
                        if i == 20:
                            o, orr = o32.next()
                            em.op("dve", lambda e, o=o, bank=bank: e.tensor_tensor(out=o[:, 0:32], in0=bank[:, 0:32], in1=prc(l, 260, 32), op=ALU.add), r=[br, "pr"], w=[orr])
                            em.op("act", lambda e, o=o: e.activation(o[:, 0:32], o[:, 0:32], AF.Exp), r=[orr], w=[orr])
                            em.op("act", lambda e, o=o: e.activation(o[:, 0:32], o[:, 0:32], AF.Ln, bias=1.0), r=[orr], w=[orr])
                            em.dma("act", P["dts"][rows, :], o[:, 0:32], r=[orr], w=[(nm, "dts", g)])

        def stage3(P, l, g, last):
            pmode[0] = 6
            cond = P["cond"]
            nm = P["name"]
            T0 = g * 512
            P["g0"] = T0
            A32.reset(); A16.reset()
            h32f, h32r = A32.take(8192)
            h32 = h32f.rearrange("p (c t) -> p c t", c=16)
            rs, rsr = A32.take(512)
            tmp32 = Ring(A32, 4, 512)
            uTf, uTr = A16.take(8192)
            uT = uTf.rearrange("p (c t) -> p c t", c=16)
            mxf, mxr = A16.take(8192)
            mx = mxf.rearrange("p (c t) -> p c t", c=16)
            sqr = Ring(A16, 2, 512)
            sgr = Ring(A16, 2, 1024)
            actr = Ring(A16, 2, 1024)
            src = P["xT"] if l == 0 else P["hT"]
            em.dma("sp", h32, src[:, :, T0:T0 + 512].rearrange("c p t -> p c t"), r=[(nm, "hT", g)], w=[h32r])
            em.dma("sp", mx, P["mixT"][:, :, T0:T0 + 512].rearrange("c p t -> p c t"), r=[(nm, "mixT")], w=[mxr])
            for j in range(8):
                wb, wres = load_w(l * NBL + 21 + j)
                wv = wb.rearrange("p (k c) -> p k c", k=16)
                for hb in range(2):
                    dc = 2 * j + hb
                    bank, br = psn()
                    for k in range(16):
                        mm(bank, wv[:, k, hb * 128:(hb + 1) * 128], mx[:, k, :], k == 0, k == 15, [wres, mxr], [br])
                    em.op("dve", lambda e, bank=bank, dc=dc: e.scalar_tensor_tensor(out=h32[:, dc, :], in0=bank, scalar=modc(l, 2, dc, cond), in1=h32[:, dc, :], op0=ALU.mult, op1=ALU.add), r=[br, "mods", h32r], w=[h32r])
            norm_group(P, h32, h32r, uT, uTr, lambda c: a12v[:, l, 1, c, cond:cond + 1], lambda c: modc(l, 3, c, cond), tmp32, sqr, rs, rsr)
            for j in range(22):
                wg, wgr = load_w(l * NBL + 29 + 3 * j)
                wu, wur = load_w(l * NBL + 30 + 3 * j)
                wd, wdr = load_w(l * NBL + 31 + 3 * j)
                wgv = wg.rearrange("p (k c) -> p k c", k=16)
                wuv = wu.rearrange("p (k c) -> p k c", k=16)
                wdv = wd.rearrange("p (k c) -> p k c", k=2)
                sg, sgres = sgr.next()
                av, avr = actr.next()
                for hb in range(2):
                    bank, br = psn()
                    for k in range(16):
                        mm(bank, wgv[:, k, hb * 128:(hb + 1) * 128], uT[:, k, :], k == 0, k == 15, [wgr, uTr], [br])
                    em.op("act", lambda e, sg=sg, bank=bank, hb=hb: e.activation(sg[:, hb * 512:(hb + 1) * 512], bank, AF.Silu), r=[br], w=[sgres])
                    bank2, br2 = psn()
                    for k in range(16):
                        mm(bank2, wuv[:, k, hb * 128:(hb + 1) * 128], uT[:, k, :], k == 0, k == 15, [wur, uTr], [br2])
                    em.op("dve", lambda e, sg=sg, av=av, bank2=bank2, hb=hb: e.tensor_tensor(out=av[:, hb * 512:(hb + 1) * 512], in0=bank2, in1=sg[:, hb * 512:(hb + 1) * 512], op=ALU.mult), r=[br2, sgres], w=[avr])
                for dc in range(16):
                    bank, br = psn()
                    for k in range(2):
                        mm(bank, wdv[:, k, dc * 128:(dc + 1) * 128], av[:, k * 512:(k + 1) * 512], k == 0, k == 1, [wdr, avr], [br])
                    em.op("dve", lambda e, bank=bank, dc=dc: e.scalar_tensor_tensor(out=h32[:, dc, :], in0=bank, scalar=modc(l, 5, dc, cond), in1=h32[:, dc, :], op0=ALU.mult, op1=ALU.add), r=[br, "mods", h32r], w=[h32r])
            if not last:
                em.dma("act", P["hT"][:, :, T0:T0 + 512].rearrange("c p t -> p c t"), h32, r=[h32r], w=[(nm, "hT", g)])
            else:
                norm_group(P, h32, h32r, None, None, None, None, tmp32, sqr, rs, rsr)

        def ctx_prep(P, l):
            A32.reset(); A16.reset()
            T = P["T"]
            nm = P["name"]
            for (srcK, nk, dstK, dk) in ((cakT, 2, P["kaT"], "kaT"), (cdkT, 4, P["kdT"], "kdT")):
                for h in range(nk):
                    a, ar = A32.take(256)
                    b, brr = A16.take(256)
                    em.dma("sp", a, srcK[l, h], w=[ar])
                    copy(b, a, [ar], [brr])
                    em.dma("act", dstK[h, :, T:T + 256], b, r=[brr], w=[(nm, dk, "ctx")])
            for (srcV, w_, dstV, dv) in ((cav, 256, P["va"], "va"), (cdv, 512, P["vd"], "vd")):
                for tt in range(2):
                    a, ar = A32.take(512)
                    b, brr = A16.take(512)
                    em.dma("sp", a[:, 0:w_], srcV[l, tt * 128:(tt + 1) * 128, :], w=[ar])
                    copy(b[:, 0:w_], a[:, 0:w_], [ar], [brr])
                    em.dma("act", dstV[T + tt * 128:T + tt * 128 + 128, :], b[:, 0:w_], r=[brr], w=[(nm, dv, "ctx")])

        def attn(P, l):
            smp = P["sample"]
            nm = P["name"]
            T, L = P["T"], P["L"]
            TK = T + (256 if smp else 0)
            NKT = TK // 128
            A32.reset(); A16.reset()
            kvb = [(A16.take(TK), A16.take(TK)) for _ in range(2)]
            qb = [A16.take(2 * T) for _ in range(1)]
            ptr = Ring(A16, 3, 512)
            outr = Ring(A16, 2, 512)
            f32r = Ring(A32, 6, 512)
            mrep = lambda m: m.unsqueeze(1).to_broadcast([128, 2, 128])
            pmode[0] = 3
            itc = [0]
            jobs = [("A", g) for g in range(2)] + [("B", h) for h in range(4)]
            import os as _os
            if _os.environ.get("ATJ"):
                jobs = [j for j in jobs if j[0] in _os.environ["ATJ"]]
            for ji, (kind, hh) in enumerate(jobs):
                (kT, kTr), (vv, vvr) = kvb[ji % 2]
                vvv = vv.rearrange("p (k d) -> p k d", d=128)
                q, qr = qb[0]
                if kind == "A":
                    em.dma("sp", kT, P["kaT"][hh], r=[(nm, "kaT")], w=[kTr])
                    em.dma("sp", vvv, P["va"][:, hh * 128:(hh + 1) * 128].rearrange("(k p) d -> p k d", p=128), r=[(nm, "va")], w=[vvr])
                    qv = q.rearrange("p (h t) -> p h t", h=2)
                    em.dma("sp", qv, P["qaT"][2 * hh:2 * hh + 2].rearrange("h p t -> p h t"), r=[(nm, "qaT")], w=[qr])
                    scale = 128 ** -0.5
                    for qt in range(T // 128):
                        seq0 = (qt * 128 // L) * (L // 128)
                        if smp:
                            keys = [(kt, m) for kt, m in ((qt - 1, mB_b), (qt, None), (qt + 1, mF_b)) if 0 <= kt < T // 128]
                            keys += [(T // 128, None), (T // 128 + 1, None)]
                        else:
                            keys = [(seq0 + k, None) for k in range(L // 128)]
                        itc[0] += 1
                        ob, obr = psl(3 + itc[0] % 2)
                        sb_, sbr = psl(5)
                        for ki, (kt, m) in enumerate(keys):
                            sbk, sr = psn()
                            mm(sbk[:, 0:256], kT[:, kt * 128:(kt + 1) * 128], qv[:, :, qt * 128:(qt + 1) * 128], True, m is None, [kTr, qr], [sr])
                            if m is not None:
                                mm(sbk[:, 0:256], ident_b, mrep(m), False, True, CB, [sr])
                            p_, pr_ = ptr.next()
                            em.op("act", lambda e, p_=p_, sbk=sbk, scale=scale: e.activation(p_[:, 0:256], sbk[:, 0:256], AF.Exp, scale=scale), r=[sr], w=[pr_])
                            mm(ob[:, 0:256], vvv[:, kt, :], p_[:, 0:256], ki == 0, ki == len(keys) - 1, [vvr, pr_], [obr])
                            mm(sb_[:, 0:256], ones_b, p_[:, 0:256], ki == 0, ki == len(keys) - 1, [pr_] + CB, [sbr])
                        den, denr = f32r.next()
                        es = esink(l)[:, 2 * hh:2 * hh + 2].unsqueeze(2).to_broadcast([128, 2, 128])
                        em.op("dve", lambda e, den=den, sb_=sb_, es=es: e.tensor_tensor(out=den[:, 0:256].rearrange("p (h t) -> p h t", h=2), in0=sb_[:, 0:256].rearrange("p (h t) -> p h t", h=2), in1=es, op=ALU.add), r=[sbr] + SMP, w=[denr])
                        em.op("dve", lambda e, den=den: e.reciprocal(den[:, 0:256], den[:, 0:256]), r=[denr], w=[denr])
                        o, orr = outr.next()
                        em.op("dve", lambda e, o=o, ob=ob, den=den: e.tensor_tensor(out=o[:, 0:256], in0=ob[:, 0:256], in1=den[:, 0:256], op=ALU.mult), r=[obr, denr], w=[orr])
                        em.dma("act", P["mixT"][2 * hh:2 * hh + 2, :, qt * 128:(qt + 1) * 128].rearrange("h p t -> p h t"), o[:, 0:256].rearrange("p (h t) -> p h t", h=2), r=[orr], w=[(nm, "mixT")])
                else:
                    em.dma("sp", kT, P["kdT"][hh], r=[(nm, "kdT")], w=[kTr])
                    em.dma("sp", vvv, P["vd"][:, hh * 128:(hh + 1) * 128].rearrange("(k p) d -> p k d", p=128), r=[(nm, "vd")], w=[vvr])
                    em.dma("sp", q[:, 0:T], P["qdT"][hh], r=[(nm, "qdT")], w=[qr])
                    scale = 64 ** -0.5
                    for qb_ in range(T // 256):
                        if smp:
                            keys = list(range(NKT))
                        else:
                            s0 = (qb_ * 256 // L) * (L // 128)
                            keys = [s0 + k for k in range(L // 128)]
                        itc[0] += 1
                        ob, obr = psl(3 + itc[0] % 2)
                        sb_, sbr = psl(5)
                        for ki, kt in enumerate(keys):
                            p_, pr_ = ptr.next()
                            for j in range(2):
                                sbk, sr = psn()
                                mm(sbk[:, 0:256], kT[64 * j:64 * j + 64, kt * 128:(kt + 1) * 128], q[64 * j:64 * j + 64, qb_ * 256:(qb_ + 1) * 256], True, True, [kTr, qr], [sr])
                                em.op("act", lambda e, p_=p_, sbk=sbk, scale=scale, j=j: e.activation(p_[:, j * 256:(j + 1) * 256], sbk[:, 0:256], AF.Exp, scale=scale), r=[sr], w=[pr_])
                            mm(ob, vvv[:, kt, :], p_, ki == 0, ki == len(keys) - 1, [vvr, pr_], [obr])
                            mm(sb_, ones_b, p_, ki == 0, ki == len(keys) - 1, [pr_] + CB, [sbr])
                        den, denr = f32r.next()
                        em.op("dve", lambda e, den=den, sb_=sb_: e.reciprocal(den, sb_), r=[sbr], w=[denr])
                        t1, t1r = f32r.next()
                        em.op("dve", lambda e, t1=t1, ob=ob, den=den: e.tensor_tensor(out=t1, in0=ob, in1=den, op=ALU.mult), r=[obr, denr], w=[t1r])
                        od, odr = f32r.next()
                        em.op("dve", lambda e, od=od, t1=t1: e.scalar_tensor_tensor(out=od[:, 0:256], in0=t1[:, 256:512], scalar=nlam_ap(l), in1=t1[:, 0:256], op0=ALU.mult, op1=ALU.add), r=[t1r] + SMP, w=[odr])
                        em.op("act", lambda e, od=od: e.activation(od[:, 256:512], od[:, 0:256], AF.Square), r=[odr], w=[odr])
                        nb_, nbr = psn()
                        mm(nb_[:, 0:256], ones32, od[:, 256:512], True, True, [odr] + SMC, [nbr])
                        em.op("act", lambda e, den=den, nb_=nb_: e.activation(den[:, 0:256], nb_[:, 0:256], AF.Sqrt, bias=epsT, scale=1.0 / 128), r=[nbr] + SMC, w=[denr])
                        em.op("dve", lambda e, den=den: e.reciprocal(den[:, 0:256], den[:, 0:256]), r=[denr], w=[denr])
                        o, orr = outr.next()
                        em.op("dve", lambda e, o=o, od=od, den=den: e.scalar_tensor_tensor(out=o[:, 0:256], in0=od[:, 0:256], scalar=dnl_ap(l), in1=den[:, 0:256], op0=ALU.mult, op1=ALU.mult), r=[odr, denr] + SMP, w=[orr])
                        em.dma("act", P["mixT"][4 + hh, :, qb_ * 256:(qb_ + 1) * 256], o[:, 0:256], r=[orr], w=[(nm, "mixT")])

        def ssd(P, l, bidx):
            smp = P["sample"]
            nm = P["name"]
            T, L, nseq = P["T"], P["L"], P["nseq"]
            pmode[0] = 3
            A32.reset(); A16.reset()
            xinr = Ring(A32, 2, 6 * 516)
            accr = Ring(A32, 2, 6 * 512)
            xcsr = Ring(A16, 2, 6 * 512)
            tmr = Ring(A16, 3, 640)
            for grp in range(2):
                blks = [4 * grp + b for b in range(4)] + [8 + grp, 10 + grp]
                for sq in range(nseq):
                    for t0 in range(0, L, 512):
                        n = min(512, L - t0)
                        base = sq * L + t0
                        lo = 2 if t0 > 0 else 0
                        hi = 2 if t0 + n < L else 0
                        xf, xr = xinr.next()
                        xin = xf.rearrange("p (j t) -> p j t", j=6)
                        if lo == 0:
                            em.op("pool", lambda e, xin=xin: e.memset(xin[:, :, 0:2], 0.0), w=[xr])
                        if hi == 0:
                            em.op("pool", lambda e, xin=xin, n=n: e.memset(xin[:, :, 2 + n:4 + n], 0.0), w=[xr])
                        for j, blk in enumerate(blks):
                            em.dma("sp", xin[:, j, 2 - lo:2 + n + hi], P["xcT"][blk, :, base - lo:base + n + hi], r=[(nm, "xcT", (base // 512))], w=[xr])
                        af, ar = accr.next()
                        acc = af.rearrange("p (j t) -> p j t", j=6)
                        xsf, xsr = xcsr.next()
                        xcs = xsf.rearrange("p (j t) -> p j t", j=6)
                        for j, blk in enumerate(blks):
                            cw = lambda k, blk=blk: pfc(l, 128 + blk * 5 + k)
                            em.op("dve", lambda e, acc=acc, xin=xin, j=j, cw=cw, blk=blk, n=n: e.tensor_scalar(acc[:, j, 0:n], xin[:, j, 0:n], cw(0), pfc(l, 188 + blk), ALU.mult, ALU.add), r=[xr, "pf"], w=[ar])
                            for k in range(1, 5):
                                em.op("dve", lambda e, acc=acc, xin=xin, j=j, cw=cw, k=k, n=n: e.scalar_tensor_tensor(out=acc[:, j, 0:n], in0=xin[:, j, k:k + n], scalar=cw(k), in1=acc[:, j, 0:n], op0=ALU.mult, op1=ALU.add), r=[xr, ar, "pf"], w=[ar])
                        em.op("act", lambda e, xcs=xcs, acc=acc, n=n: e.activation(xcs[:, :, 0:n], acc[:, :, 0:n], AF.Silu), r=[ar], w=[xsr])
                        em.dma("act", P["bT"][grp, :, base:base + n], xcs[:, 4, 0:n], r=[xsr], w=[(nm, "bT")])
                        em.dma("act", P["cT"][grp, :, base:base + n], xcs[:, 5, 0:n], r=[xsr], w=[(nm, "cT")])
                        for tt in range(n // 128):
                            pb, pbr = ptn()
                            for b in range(4):
                                em.op("pe", lambda e, pb=pb, xcs=xcs, b=b, tt=tt: e.transpose(pb[:, b * 128:(b + 1) * 128], xcs[:, b, tt * 128:(tt + 1) * 128], ident_b), r=[xsr] + CB, w=[pbr])
                            pb2, pb2r = ptn()
                            em.op("pe", lambda e, pb2=pb2, xcs=xcs, tt=tt: e.transpose(pb2[:, 0:128], xcs[:, 4, tt * 128:(tt + 1) * 128], ident_b), r=[xsr] + CB, w=[pb2r])
                            o, orr = tmr.next()
                            copy(o[:, 0:512], pb, [pbr], [orr])
                            copy(o[:, 512:640], pb2[:, 0:128], [pb2r], [orr])
                            rows = slice(base + tt * 128, base + tt * 128 + 128)
                            em.dma("act", P["xtm"][grp, rows, :], o[:, 0:512], r=[orr], w=[(nm, "xtm")])
                            em.dma("act", P["btm"][grp, rows, :], o[:, 512:640], r=[orr], w=[(nm, "btm")])
            em.barrier()
            A32.reset(); A16.reset()
            dmat, dmr = A16.take(16 * 128)
            dmv = dmat.rearrange("p (h c) -> p h c", h=16)
            for h in range(16):
                em.op("dve", lambda e, h=h: e.tensor_scalar(dmv[:, h, :], ident_b, prc(l, 324 + h), None, ALU.mult), r=CB + ["pr"], w=[dmr])
            stf = [A32.take(512) for _ in range(2)]
            ldr32 = Ring(A32, 2, 512 + 32)
            ldr16 = Ring(A16, 2, 512 + 128 + 128 + 128)
            hbr = Ring(A16, 2, 512)
            smr = Ring(A32, 3, 128)
            cbr = Ring(A32, 2, 128)
            er = Ring(A32, 3, 128)
            wtr = Ring(A16, 4, 128)
            d16r_ = Ring(A16, 2, 16)
            xsr_ = Ring(A16, 2, 512)
            hfr = Ring(A16, 2, 512)
            t32 = Ring(A32, 4, 512)
            ynr = Ring(A16, 2, 512)
            mtr = Ring(A16, 2, 512)
            nch = L // 128

            def load_chunk(grp, ct, need_z):
                rows = slice(ct * 128, ct * 128 + 128)
                a, ar = ldr32.next()
                b, brr = ldr16.next()
                em.dma("sp", a[:, 512:544], P["dts"][rows, :], r=[(nm, "dts", ct // 4)], w=[ar])
                if need_z:
                    em.dma("sp", a[:, 0:512], P["zs"][rows, grp * 512:(grp + 1) * 512], r=[(nm, "zs", ct // 4)], w=[ar])
                em.dma("sp", b[:, 0:512], P["xtm"][grp, rows, :], r=[(nm, "xtm")], w=[brr])
                em.dma("sp", b[:, 512:640], P["btm"][grp, rows, :], r=[(nm, "btm")], w=[brr])
                if need_z:
                    em.dma("sp", b[:, 640:768], P["bT"][grp, :, rows], r=[(nm, "bT")], w=[brr])
                    em.dma("sp", b[:, 768:896], P["cT"][grp, :, rows], r=[(nm, "cT")], w=[brr])
                return a, ar, b, brr

            def state_update(stt, sttr, xt, btm_, lr16, te_src_tot, cum_sb, cum_r, dtc, lr32, first_zero, tot_ps, tot_r, col0):
                s_, s_r = smr.next()
                em.op("dve", lambda e: e.tensor_tensor(out=s_[:, 0:8], in0=tot_ps[:, col0:col0 + 8], in1=cum_sb, op=ALU.subtract), r=[tot_r, cum_r], w=[s_r])
                em.op("act", lambda e: e.activation(s_[:, 0:8], s_[:, 0:8], AF.Exp), r=[s_r], w=[s_r])
                em.op("dve", lambda e: e.tensor_tensor(out=s_[:, 0:8], in0=s_[:, 0:8], in1=dtc, op=ALU.mult), r=[s_r, lr32], w=[s_r])
                em.op("act", lambda e: e.activation(s_[:, 8:16], tot_ps[:, col0:col0 + 8], AF.Exp), r=[tot_r], w=[s_r])
                xs, xsr2 = xsr_.next()
                em.op("dve", lambda e: e.tensor_tensor(out=xs.rearrange("p (h c) -> p h c", h=8), in0=xt.rearrange("p (h c) -> p h c", h=8), in1=s_[:, 0:8].unsqueeze(2).to_broadcast([128, 8, 64]), op=ALU.mult), r=[lr16, s_r], w=[xsr2])
                sp_, spr = psn()
                mm(sp_, btm_, xs, True, True, [lr16, xsr2], [spr])
                em.op("dve", lambda e: e.tensor_tensor(out=stt.rearrange("p (h c) -> p h c", h=8), in0=stt.rearrange("p (h c) -> p h c", h=8), in1=s_[:, 8:16].unsqueeze(2).to_broadcast([128, 8, 64]), op=ALU.mult), r=[sttr, s_r], w=[sttr])
                em.op("dve", lambda e: e.tensor_tensor(out=stt, in0=stt, in1=sp_, op=ALU.add), r=[sttr, spr], w=[sttr])

            for grp in range(2):
                for sq in range(nseq):
                    c0 = sq * nch
                    (sf, sfr), (sbk_, sbkr) = stf
                    if smp:
                        em.dma("sp", sbk_, s0b[l, :, grp * 512:(grp + 1) * 512], w=[sbkr])
                    else:
                        em.op("pool", lambda e: e.memset(sbk_, 0.0), w=[sbkr])
                    for c in range(nch - 1, -1, -1):
                        ct = c0 + c
                        a, ar, b, brr = load_chunk(grp, ct, False)
                        hb16, hb16r = hbr.next()
                        copy(hb16, sbk_, [sbkr], [hb16r])
                        em.dma("act", P["hbp"][grp, ct], hb16, r=[hb16r], w=[(nm, "hbp")])
                        dcol = 512 + 16 + 8 * grp
                        d16, d16r = d16r_.next()
                        em.op("dve", lambda e, d16=d16, a=a, dcol=dcol, grp=grp: e.tensor_tensor(out=d16[:, 0:8], in0=a[:, dcol:dcol + 8], in1=aneg(l)[:, 16 + 8 * grp:24 + 8 * grp], op=ALU.mult), r=[ar] + SMP, w=[d16r])
                        pp, ppr = psl(5)
                        mm(pp[:, 0:8], triR_b, d16[:, 0:8], True, True, [d16r] + CB, [ppr])
                        mm(pp[:, 16:24], ones_b, d16[:, 0:8], True, True, [d16r] + CB, [ppr])
                        cs, csr = smr.next()
                        em.op("dve", lambda e, cs=cs, pp=pp: e.tensor_copy(cs[:, 0:32], pp[:, 0:32]), r=[ppr], w=[csr])
                        state_update(sbk_, sbkr, b[:, 0:512], b[:, 512:640], brr, None, cs[:, 0:8], csr, a[:, dcol:dcol + 8], ar, False, cs, csr, 16)
                    if not smp:
                        em.dma("act", nsb[sq, l, :, grp * 512:(grp + 1) * 512], sbk_, r=[sbkr], final=True)
                    if smp:
                        em.dma("sp", sf, s0f[l, :, grp * 512:(grp + 1) * 512], w=[sfr])
                    else:
                        em.op("pool", lambda e: e.memset(sf, 0.0), w=[sfr])
                    for c in range(nch):
                        ct = c0 + c
                        a, ar, b, brr = load_chunk(grp, ct, True)
                        xt = b[:, 0:512]
                        hbl, hblr = hbr.next()
                        em.dma("sp", hbl, P["hbp"][grp, ct], r=[(nm, "hbp")], w=[hblr])
                        fcol = 512 + 8 * grp
                        bcol = 512 + 16 + 8 * grp
                        d16, d16r = d16r_.next()
                        em.op("dve", lambda e, d16=d16, a=a, fcol=fcol, grp=grp: e.tensor_tensor(out=d16[:, 0:8], in0=a[:, fcol:fcol + 8], in1=aneg(l)[:, 8 * grp:8 * grp + 8], op=ALU.mult), r=[ar] + SMP, w=[d16r])
                        em.op("dve", lambda e, d16=d16, a=a, bcol=bcol, grp=grp: e.tensor_tensor(out=d16[:, 8:16], in0=a[:, bcol:bcol + 8], in1=aneg(l)[:, 16 + 8 * grp:24 + 8 * grp], op=ALU.mult), r=[ar] + SMP, w=[d16r])
                        pp, ppr = psl(5)
                        mm(pp[:, 0:8], tri_b, d16[:, 0:8], True, True, [d16r] + CB, [ppr])
                        mm(pp[:, 8:16], triR_b, d16[:, 8:16], True, True, [d16r] + CB, [ppr])
                        mm(pp[:, 16:32], ones_b, d16[:, 0:16], True, True, [d16r] + CB, [ppr])
                        cs, csr = smr.next()
                        em.op("dve", lambda e, cs=cs, pp=pp: e.tensor_copy(cs[:, 0:16], pp[:, 0:16]), r=[ppr], w=[csr])
                        em.op("dve", lambda e, cs=cs, pp=pp: e.tensor_copy(cs[:, 64:80], pp[:, 16:32]), r=[ppr], w=[csr])
                        em.op("dve", lambda e, cs=cs: e.tensor_scalar(cs[:, 16:32], cs[:, 0:16], -1.0, None, ALU.mult), r=[csr], w=[csr])
                        em.op("act", lambda e, cs=cs: e.activation(cs[:, 32:48], cs[:, 0:16], AF.Exp), r=[csr], w=[csr])
                        cbp, cbpr = psn()
                        mm(cbp[:, 0:128], b[:, 640:768], b[:, 768:896], True, True, [brr], [cbpr])
                        cbt, cbtr = cbr.next()
                        copy(cbt, cbp[:, 0:128], [cbpr], [cbtr])
                        yb, ybr = psl(3 + c % 2)
                        for hh in range(8):
                            wts = []
                            for di in range(2):
                                col = 8 * di + hh
                                sp_, spr = psn()
                                mm(sp_[:, 0:128], d16[:, col:col + 1].to_broadcast([128, 128]), tri_b if di == 0 else triR_b, True, False, [d16r] + CB, [spr])
                                mm(sp_[:, 0:128], ident_b, mF_b if di == 0 else mB_b, False, True, CB, [spr])
                                e_, e_r = er.next()
                                em.op("act", lambda e, e_=e_, sp_=sp_, cs=cs, col=col: e.activation(e_, sp_[:, 0:128], AF.Exp, bias=cs[:, 16 + col:17 + col]), r=[spr, csr], w=[e_r])
                                w_, w_r = wtr.next()
                                dtcol = (fcol if di == 0 else bcol) + hh
                                em.op("dve", lambda e, w_=w_, e_=e_, a=a, dtcol=dtcol, cbt=cbt: e.scalar_tensor_tensor(out=w_, in0=e_, scalar=a[:, dtcol:dtcol + 1], in1=cbt, op0=ALU.mult, op1=ALU.mult), r=[e_r, ar, cbtr], w=[w_r])
                                wts.append((w_, w_r))
                            xh = xt[:, hh * 64:(hh + 1) * 64]
                            mm(yb[:, hh * 64:(hh + 1) * 64], wts[0][0], xh, True, False, [wts[0][1], brr], [ybr])
                            mm(yb[:, hh * 64:(hh + 1) * 64], wts[1][0], xh, False, False, [wts[1][1], brr], [ybr])
                            mm(yb[:, hh * 64:(hh + 1) * 64], dmv[:, 8 * grp + hh, :], xh, False, True, [dmr, brr], [ybr])
                        hf16, hf16r = hfr.next()
                        copy(hf16, sf, [sfr], [hf16r])
                        of_, ofr = psn()
                        mm(of_, b[:, 768:896], hf16, True, True, [brr, hf16r], [ofr])
                        ob_, obr_ = psn()
                        mm(ob_, b[:, 768:896], hbl, True, True, [brr, hblr], [obr_])
                        t, tr = t32.next()
                        t2, t2r = t32.next()
                        v8 = lambda x: x.rearrange("p (h c) -> p h c", h=8)
                        em.op("dve", lambda e, t=t, of_=of_, cs=cs: e.tensor_tensor(out=v8(t), in0=v8(of_), in1=cs[:, 32:40].unsqueeze(2).to_broadcast([128, 8, 64]), op=ALU.mult), r=[ofr, csr], w=[tr])
                        em.op("dve", lambda e, t2=t2, ob_=ob_, cs=cs: e.tensor_tensor(out=v8(t2), in0=v8(ob_), in1=cs[:, 40:48].unsqueeze(2).to_broadcast([128, 8, 64]), op=ALU.mult), r=[obr_, csr], w=[t2r])
                        em.op("pool", lambda e, t=t, t2=t2: e.tensor_tensor(out=t, in0=t, in1=t2, op=ALU.add), r=[tr, t2r], w=[tr])
                        em.op("dve", lambda e, t=t, yb=yb: e.tensor_tensor(out=t, in0=t, in1=yb, op=ALU.add), r=[tr, ybr], w=[tr])
                        em.op("dve", lambda e, t=t, a=a: e.tensor_tensor(out=t, in0=t, in1=a[:, 0:512], op=ALU.mult), r=[tr, ar], w=[tr])
                        ss, ssr = smr.next()
                        em.op("act", lambda e, t=t, t2=t2, ss=ss: e.activation(t2, t, AF.Square, accum_out=ss[:, 0:1]), r=[tr], w=[t2r, ssr])
                        em.op("act", lambda e, ss=ss: e.activation(ss[:, 1:2], ss[:, 0:1], AF.Sqrt, bias=epsT, scale=1.0 / 512), r=[ssr] + SMC, w=[ssr])
                        em.op("dve", lambda e, ss=ss: e.reciprocal(ss[:, 1:2], ss[:, 1:2]), r=[ssr], w=[ssr])
                        yn, ynr_ = ynr.next()
                        em.op("dve", lambda e, yn=yn, t=t, ss=ss: e.tensor_scalar(yn, t, ss[:, 1:2], None, ALU.mult), r=[tr, ssr], w=[ynr_])
                        pb, pbr = ptn()
                        for bq in range(4):
                            em.op("pe", lambda e, pb=pb, yn=yn, bq=bq: e.transpose(pb[:, bq * 128:(bq + 1) * 128], yn[:, bq * 128:(bq + 1) * 128], ident_b), r=[ynr_] + CB, w=[pbr])
                        mt, mtr_ = mtr.next()
                        for bq in range(4):
                            em.op("dve", lambda e, mt=mt, pb=pb, bq=bq, grp=grp: e.tensor_scalar(mt[:, bq * 128:(bq + 1) * 128], pb[:, bq * 128:(bq + 1) * 128], pfc(l, 200 + 4 * grp + bq), None, ALU.mult), r=[pbr, "pf"], w=[mtr_])
                        em.dma("act", P["mixT"][8 + 4 * grp:12 + 4 * grp, :, ct * 128:(ct + 1) * 128].rearrange("c p t -> p c t"), mt.rearrange("p (c t) -> p c t", c=4), r=[mtr_], w=[(nm, "mixT")])
                        state_update(sf, sfr, xt, b[:, 512:640], brr, None, cs[:, 0:8], csr, a[:, fcol:fcol + 8], ar, False, cs, csr, 64)
                    if not smp:
                        em.dma("act", nsf[sq, l, :, grp * 512:(grp + 1) * 512], sf, r=[sfr], final=True)

        em.barrier()
        for P in paths:
            ng = P["T"] // 512
            for l in range(depth):
                for g in range(ng):
                    if 's1' in stages:
                        stage1(P, l, g)
                        em.barrier()
                if P["sample"] and 'ctx' in stages:
                    ctx_prep(P, l)
                    em.barrier()
                if 'attn' in stages:
                    attn(P, l)
                    em.barrier()
                if 'ssd' in stages:
                    ssd(P, l, 0)
                    em.barrier()
                for g in range(ng):
                    if 's3' in stages:
                        stage3(P, l, g, l == depth - 1)
                        em.barrier()
        em.emit(nc, st)
    return nc, em


def host_prep(inp):
    f = lambda a: np.ascontiguousarray(np.asarray(a, dtype=np.float32))
    w_in, w_out, w_gu, w_dn, w_ada = f(inp["w_in"]), f(inp["w_out"]), f(inp["w_gate_up"]), f(inp["w_down"]), f(inp["w_ada"])

    def colblk(W, c0, n):
        blk = np.zeros((128, 16, 256), np.float32)
        blk[:, :, :n] = W[:, c0:c0 + n].reshape(16, 128, n).transpose(1, 0, 2)
        return blk.reshape(128, 4096)

    wf = np.zeros((DEPTH * NBL, 128, 4096), np.float32)
    wada = np.zeros((DEPTH * 48, 128, 4096), np.float32)
    for l in range(RUN_DEPTH):
        b = l * NBL
        for i in range(20):
            wf[b + i] = colblk(w_in[l], 256 * i, 256)
        wf[b + 20] = colblk(w_in[l], 5120, 32)
        for j in range(8):
            wf[b + 21 + j] = colblk(w_out[l], 256 * j, 256)
        for j in range(22):
            wf[b + 29 + 3 * j] = colblk(w_gu[l], 256 * j, 256)
            wf[b + 30 + 3 * j] = colblk(w_gu[l], 5632 + 256 * j, 256)
            wf[b + 31 + 3 * j] = w_dn[l][256 * j:256 * j + 256].reshape(2, 128, 2048).transpose(1, 0, 2).reshape(128, 4096)
        for j in range(48):
            wada[l * 48 + j] = colblk(w_ada[l], 256 * j, 256)
    fm = lambda v: f(v).reshape(-1, 128).T
    pfm = np.zeros((128, DEPTH * PFL + 16), np.float32)
    prep = np.zeros((128, DEPTH * PRL), np.float32)
    for l in range(DEPTH):
        o = l * PFL
        pfm[:, o:o + 16] = fm(inp["norm_mix"][l])
        pfm[:, o + 16:o + 32] = fm(inp["norm_ffn"][l])
        pfm[:, o + 32:o + 128] = fm(inp["b_ada"][l])
        cw = f(inp["conv_w"][l])
        pfm[:, o + 128:o + 188] = cw.reshape(5, 12, 128).transpose(2, 1, 0).reshape(128, 60)
        pfm[:, o + 188:o + 200] = fm(inp["conv_b"][l])
        pfm[:, o + 200:o + 208] = fm(inp["ssm_norm"][l])
        pfm[:, o + 208] = f(inp["diff_norm"][l])
        r = l * PRL
        prep[:, r:r + 4] = f(inp["attn_sink"][l])[None]
        prep[:, r + 4:r + 260] = f(inp["diff_lambda"][l]).reshape(1, 256)
        prep[:, r + 260:r + 292] = f(inp["dt_bias"][l]).reshape(1, 32)
        prep[:, r + 292:r + 324] = f(inp["a_log"][l]).reshape(1, 32)
        prep[:, r + 324:r + 340] = f(inp["d_skip"][l])[None]
    pfm[:, DEPTH * PFL:] = fm(inp["norm_final"])
    i128 = np.arange(128)
    ident = np.eye(128, dtype=np.float32)
    tri = (i128[:, None] <= i128[None, :]).astype(np.float32)
    triR = (i128[:, None] >= i128[None, :]).astype(np.float32)
    mF = np.where(i128[:, None] <= i128[None, :], 0.0, NEG).astype(np.float32)
    mB = np.where(i128[:, None] >= i128[None, :], 0.0, NEG).astype(np.float32)
    pswA = np.zeros((128, 128), np.float32)
    pswA[(i128 + 64) % 128, i128] = 1.0
    pswB = np.zeros((128, 128), np.float32)
    pswB[(i128 // 64) * 64 + ((i128 % 64) + 32) % 64, i128] = 1.0
    cst = np.concatenate([ident, tri, triR, mF, mB, pswA, pswB], axis=1)
    n = 4096
    row = np.repeat(np.arange(n // 64), 64).astype(np.float32)
    col = (np.arange(n) % 64).astype(np.float32)

    def tables(rot):
        q = rot // 4
        inv = (10000.0 ** (-np.arange(q, dtype=np.float32) / q)).astype(np.float32)
        ang = np.concatenate([row[:, None] * inv, col[:, None] * inv], axis=-1).astype(np.float32)
        return np.cos(ang).astype(np.float32), np.sin(ang).astype(np.float32)

    cA, sA = tables(128)
    cB, sB = tables(64)
    rope = np.zeros((4, 128, n), np.float32)
    rope[0] = np.concatenate([cA.T, cA.T], 0)
    rope[1] = np.concatenate([-sA.T, sA.T], 0)
    rope[2] = np.concatenate([cB.T, cB.T, cB.T, cB.T], 0)
    rope[3] = np.concatenate([-sB.T, sB.T, -sB.T, sB.T], 0)
    shared = dict(wf=wf, wada=wada, pfm=pfm, prep=prep, cst=cst, rope=rope)
    return shared


RUN_DEPTH = DEPTH
RUN_SAMPLE = True
RUN_STAGES = ('s1', 'ctx', 'attn', 'ssd', 's3')
RUN_PRO = (1, 1)
RUN_CORES = list(range(8))


def kernel(**inp):
    shared = host_prep(inp)
    f = lambda a: np.asarray(a, dtype=np.float32)
    xp, xs = f(inp["x_prompt"]), f(inp["x_sample"])
    maps = []
    for i in RUN_CORES:
        sidx = i % 2
        m = dict(shared)
        m["xpT"] = np.ascontiguousarray(xp[2 * i:2 * i + 2].reshape(512, 16, 128).transpose(1, 2, 0))
        m["xsT"] = np.ascontiguousarray(xs[sidx].reshape(4096, 16, 128).transpose(1, 2, 0))
        m["cakT"] = np.ascontiguousarray(f(inp["cache_attn_k"])[sidx].transpose(0, 2, 3, 1))
        m["cav"] = np.ascontiguousarray(f(inp["cache_attn_v"])[sidx].reshape(4, 256, 256))
        m["cdkT"] = np.ascontiguousarray(f(inp["cache_diff_k"])[sidx].transpose(0, 2, 3, 1))
        m["cdv"] = np.ascontiguousarray(f(inp["cache_diff_v"])[sidx].reshape(4, 256, 512))
        m["s0f"] = np.ascontiguousarray(f(inp["state_ssm_fwd"])[sidx].reshape(4, 1024, 128).transpose(0, 2, 1))
        m["s0b"] = np.ascontiguousarray(f(inp["state_ssm_bwd"])[sidx].reshape(4, 1024, 128).transpose(0, 2, 1))
        cond = np.stack([f(inp["c_ctx"]), f(inp["c"])[sidx]], axis=-1)
        m["condfm"] = np.ascontiguousarray(cond.reshape(16, 128, 2).transpose(1, 0, 2).reshape(128, 32))
        maps.append(m)
    nc, em = build(depth=RUN_DEPTH, do_sample=RUN_SAMPLE, stages=RUN_STAGES, pro=RUN_PRO)
    res = run_bass_kernel_spmd(nc, maps, core_ids=list(range(len(RUN_CORES)))).results
    if len(res) < 8:
        res = list(res) + [res[0]] * (8 - len(res))
    unT = lambda a: np.ascontiguousarray(a.transpose(2, 0, 1).reshape(a.shape[2], 2048))
    y_prompt = np.stack([unT(r["ypT"]).reshape(2, 256, 2048) for r in res]).reshape(16, 256, 2048)
    if RUN_SAMPLE:
        y_sample = np.stack([unT(res[0]["ysT"]), unT(res[1]["ysT"])])
    else:
        y_sample = np.zeros((2, 4096, 2048), np.float32)
    def cache(name, nh):
        return np.concatenate([r[name].reshape(4, 2, 256, nh, 128).transpose(1, 0, 2, 3, 4) for r in res], 0)
    def state(name):
        return np.concatenate([r[name].transpose(0, 1, 3, 2).reshape(2, 4, 16, 64, 128) for r in res], 0)
    return (y_prompt.astype(np.float32), y_sample.astype(np.float32), cache("nak", 2), cache("nav", 2), cache("ndk", 4), cache("ndv", 4), state("nsf"), state("nsb"))
```
